# Optimizing a Trainium2 kernel written in Bass

```python
import jax, jax.numpy as jnp
from jax import lax
import numpy as np

D_MODEL = 1024
BATCH = 2
SEQ = 16384
DEPTH = 4

N_META = 16
N_MIXERS = 4
D_MIX = D_MODEL
GROUP_W = D_MIX // N_MIXERS
HEAD = 64
N_H = GROUP_W // HEAD
POOL_WINDOWS = (2, 4, 8, 16)
RW_W_RANK = 32
RW_A_RANK = 32
RW_V_RANK = 32
RW_G_RANK = 64
RW_COLS = 3 * GROUP_W + RW_W_RANK + RW_A_RANK + RW_G_RANK
RW_SPLITS = (GROUP_W, 2 * GROUP_W, 3 * GROUP_W, 3 * GROUP_W + RW_W_RANK, 3 * GROUP_W + RW_W_RANK + RW_A_RANK)
LRU_C = 8.0
CONV_W = 4
ML_CHUNK = 64
ML_COLS = 4 * GROUP_W + 2 * N_H
PROJ_SPLITS = (GROUP_W, GROUP_W + RW_COLS, 2 * GROUP_W + RW_COLS, 3 * GROUP_W + RW_COLS)
D_IN = 3 * GROUP_W + RW_COLS + ML_COLS
D_FF = 4 * D_MODEL
ALPHA = (2 * DEPTH) ** 0.25
BETA = (8 * DEPTH) ** -0.25
LN_EPS = 1e-5
GN_EPS = 64e-5
NEG = -1e30

kernel_name = 'hybrid_parallel_heads_pool_rwkv7_rglru_mlstm'


def layer_norm(x, g, b):
    xf = x.astype(jnp.float32)
    mu = jnp.mean(xf, -1, keepdims=True)
    var = jnp.mean(jnp.square(xf - mu), -1, keepdims=True)
    return ((xf - mu) * lax.rsqrt(var + LN_EPS) * g + b).astype(x.dtype)


def head_norm(y, g, b):
    mu = jnp.mean(y, -1, keepdims=True)
    var = jnp.mean(jnp.square(y - mu), -1, keepdims=True)
    yn = (y - mu) * lax.rsqrt(var + GN_EPS)
    return yn.reshape(y.shape[0], y.shape[1], -1) * g + b


def split_heads(z):
    return z.reshape(z.shape[0], z.shape[1], N_H, HEAD).astype(jnp.float32)


def token_shift(z):
    return jnp.pad(z, ((0, 0), (1, 0), (0, 0)))[:, :-1]


def causal_dwconv(z, w, b):
    y = lax.conv_general_dilated(z, w[:, None, :].astype(z.dtype), window_strides=(1,),
                                 padding=[(CONV_W - 1, 0)], dimension_numbers=('NWC', 'WIO', 'NWC'),
                                 feature_group_count=z.shape[-1])
    return y + b


def pool_mixer(u, w_blk, scale):
    B, T, _ = u.shape
    uf = u.astype(jnp.float32)
    cs = jnp.pad(jnp.cumsum(uf, axis=1), ((0, 0), (1, 0), (0, 0)))
    pos = jnp.arange(T, dtype=jnp.float32)
    groups = []
    for gi, w in enumerate(POOL_WINDOWS):
        c = cs[:, :, gi * HEAD:(gi + 1) * HEAD]
        prev = jnp.pad(c, ((0, 0), (w - 1, 0), (0, 0)))[:, :T]
        mean = (c[:, 1:] - prev) / jnp.minimum(pos + 1.0, float(w))[None, :, None]
        groups.append(mean - uf[:, :, gi * HEAD:(gi + 1) * HEAD])
    d = jnp.stack(groups, axis=2)
    y = jnp.einsum('btgc,gcd->btgd', d, w_blk).reshape(B, T, GROUP_W) * scale
    return y.astype(u.dtype)


def rwkv7_mixer(p, mu, w0, w_up, a0, a_up, g_up, k_k, k_a, r_k, gn_g, gn_b, v_first, v_mix):
    B, T, _ = p.shape
    pm = p + (token_shift(p) - p) * mu
    r, k, v, wd, ad, gd = jnp.split(pm, RW_SPLITS, axis=-1)
    w = -jax.nn.softplus(-(w0 + jnp.tanh(wd) @ w_up)) - 0.5
    log_decay = -jnp.exp(w.astype(jnp.float32))
    a = jax.nn.sigmoid(a0 + ad @ a_up)
    g = jax.nn.sigmoid(gd) @ g_up
    if v_first is None:
        v_first = v
    else:
        v0, v_down, v_up = v_mix
        v = v + (v_first - v) * jax.nn.sigmoid(v0 + (v @ v_down) @ v_up)
    kk = split_heads(k * k_k)
    kk = kk * lax.rsqrt(jnp.sum(kk * kk, -1, keepdims=True) + 1e-12)
    k = k * (1.0 + (a - 1.0) * k_a)
    rh, kh, vh, ah = split_heads(r), split_heads(k), split_heads(v), split_heads(a)
    dh = jnp.exp(split_heads(log_decay))
    bvec = kk * ah
    xs = tuple(jnp.moveaxis(z, 1, 0) for z in (rh, dh, kh, vh, kk, bvec))

    def step(S, inp):
        r_t, d_t, k_t, v_t, kk_t, b_t = inp
        sa = jnp.einsum('bhvk,bhk->bhv', S, kk_t)
        S = S * d_t[:, :, None, :] - sa[..., None] * b_t[:, :, None, :] + v_t[..., None] * k_t[:, :, None, :]
        return S, jnp.einsum('bhvk,bhk->bhv', S, r_t)

    S0 = jnp.zeros((B, N_H, HEAD, HEAD), jnp.float32)
    _, y = lax.scan(step, S0, xs)
    y = head_norm(jnp.moveaxis(y, 0, 1), gn_g, gn_b)
    bonus = jnp.sum(rh * kh * r_k, -1, keepdims=True) * vh
    out = (y + bonus.reshape(B, T, GROUP_W)) * g
    return out.astype(p.dtype), v_first


def rglru_mixer(xb, gate, conv_w, conv_b, ga_w, ga_b, gx_w, gx_b, lam):
    B, T, _ = xb.shape
    xc = causal_dwconv(xb, conv_w, conv_b)
    xh = split_heads(xc)
    r = jax.nn.sigmoid(jnp.einsum('btgc,gcd->btgd', xh, ga_w).reshape(B, T, GROUP_W) + ga_b)
    i = jax.nn.sigmoid(jnp.einsum('btgc,gcd->btgd', xh, gx_w).reshape(B, T, GROUP_W) + gx_b)
    log_a = -LRU_C * r * jax.nn.softplus(-lam.astype(jnp.float32))
    a = jnp.exp(log_a)
    u = jnp.sqrt(-jnp.expm1(2.0 * log_a)) * (i * xc.astype(jnp.float32))

    def combine(lhs, rhs):
        return lhs[0] * rhs[0], rhs[0] * lhs[1] + rhs[1]

    _, h = lax.associative_scan(combine, (a, u), axis=1)
    return (h * jax.nn.gelu(gate.astype(jnp.float32))).astype(xb.dtype)


def mlstm_mixer(p, if_b, gn_g, gn_b):
    B, T, _ = p.shape
    L = ML_CHUNK
    pad = L - N_META
    Tp = T + pad
    NC = Tp // L
    pp = jnp.pad(p.astype(jnp.float32), ((0, 0), (pad, 0), (0, 0)))
    q, k, v, o, gates = jnp.split(pp, (GROUP_W, 2 * GROUP_W, 3 * GROUP_W, 4 * GROUP_W), axis=-1)

    def chunks(z):
        return z.reshape(B, NC, L, N_H, HEAD)

    q = chunks(q) * HEAD ** -0.5
    k = chunks(k)
    v = chunks(v)
    gates = (gates + if_b).reshape(B, NC, L, 2 * N_H)
    valid = (jnp.arange(Tp) >= pad).reshape(1, NC, L, 1)
    logi = jnp.where(valid, gates[..., :N_H], NEG)
    logf = jnp.where(valid, jax.nn.log_sigmoid(gates[..., N_H:]), 0.0)
    b = jnp.cumsum(logf, axis=2)
    causal = jnp.tril(jnp.ones((L, L), dtype=bool))[None, None, :, :, None]
    dmat = jnp.where(causal, b[:, :, :, None, :] - b[:, :, None, :, :] + logi[:, :, None, :, :], NEG)
    m_intra = jnp.max(dmat, axis=3)
    b_last = b[:, :, -1]
    g_loc = b_last[:, :, None] - b + logi
    m_loc = jnp.max(g_loc, axis=2)
    w_loc = jnp.exp(g_loc - m_loc[:, :, None])
    c_loc = jnp.einsum('bnlh,bnlhv,bnlhk->bnhvk', w_loc, v, k)
    n_loc = jnp.einsum('bnlh,bnlhk->bnhk', w_loc, k)

    def step(carry, inp):
        c_st, n_st, m_st = carry
        c_l, n_l, m_l, bl = inp
        m_new = jnp.maximum(bl + m_st, m_l)
        s_old = jnp.exp(bl + m_st - m_new)
        s_new = jnp.exp(m_l - m_new)
        c_new = s_old[..., None, None] * c_st + s_new[..., None, None] * c_l
        n_new = s_old[..., None] * n_st + s_new[..., None] * n_l
        return (c_new, n_new, m_new), (c_st, n_st, m_st)

    init = (jnp.zeros((B, N_H, HEAD, HEAD), jnp.float32), jnp.zeros((B, N_H, HEAD), jnp.float32),
            jnp.zeros((B, N_H), jnp.float32))
    xs = tuple(jnp.moveaxis(z, 1, 0) for z in (c_loc, n_loc, m_loc, b_last))
    _, (c_prev, n_prev, m_prev) = lax.scan(step, init, xs)
    c_prev = jnp.moveaxis(c_prev, 0, 1)
    n_prev = jnp.moveaxis(n_prev, 0, 1)
    m_prev = jnp.moveaxis(m_prev, 0, 1)
    m_inter = b + m_prev[:, :, None, :]
    m_t = jnp.maximum(m_intra, m_inter)
    s = jnp.einsum('bnthd,bnshd->bntsh', q, k) * jnp.exp(dmat - m_t[:, :, :, None, :])
    w_inter = jnp.exp(m_inter - m_t)
    num = jnp.einsum('bntsh,bnshv->bnthv', s, v) + w_inter[..., None] * jnp.einsum('bnhvk,bnthk->bnthv', c_prev, q)
    den = jnp.sum(s, axis=3) + w_inter * jnp.einsum('bnhk,bnthk->bnth', n_prev, q)
    hh = num / jnp.maximum(jnp.abs(den), jnp.exp(-m_t))[..., None]
    hh = hh.reshape(B, Tp, N_H, HEAD)[:, pad:]
    y = head_norm(hh, gn_g, gn_b) * jax.nn.sigmoid(o[:, pad:])
    return y.astype(p.dtype)


def setup_inputs(seed: int = 0) -> dict:
    key = jax.random.key(seed)
    ks = iter(jax.random.split(key, 64))

    def nrm(shape, scale):
        return scale * jax.random.normal(next(ks), shape, jnp.float32)

    def gain(shape):
        return 1.0 + nrm(shape, 0.02)

    dm1 = DEPTH - 1
    sig = jax.random.uniform(next(ks), (DEPTH, GROUP_W), jnp.float32, 0.9, 0.999) ** (1.0 / LRU_C)
    lru_lambda = jnp.log(sig) - jnp.log1p(-sig)
    ml_if_b = jnp.concatenate([nrm((DEPTH, N_H), 0.1) - 1.0,
                               jnp.broadcast_to(jnp.linspace(3.0, 6.0, N_H), (DEPTH, N_H)) + nrm((DEPTH, N_H), 0.1)], axis=-1)
    return {
        'x': nrm((BATCH, SEQ, D_MODEL), 1.0),
        'meta': nrm((N_META, D_MODEL), 1.0),
        'emb_ln_g': gain((D_MODEL,)),
        'emb_ln_b': nrm((D_MODEL,), 0.02),
        'w_in': nrm((DEPTH, D_MODEL, D_IN), D_MODEL ** -0.5),
        'w_out': nrm((DEPTH, D_MIX, D_MODEL), BETA * D_MIX ** -0.5),
        'pool_w': nrm((DEPTH, N_H, HEAD, HEAD), HEAD ** -0.5),
        'pool_scale': gain((DEPTH, GROUP_W)),
        'rw_mu': jax.random.uniform(next(ks), (DEPTH, RW_COLS), jnp.float32),
        'rw_w0': jax.random.uniform(next(ks), (DEPTH, GROUP_W), jnp.float32, -6.0, -1.0),
        'rw_w_up': nrm((DEPTH, RW_W_RANK, GROUP_W), 0.1),
        'rw_a0': nrm((DEPTH, GROUP_W), 0.1),
        'rw_a_up': nrm((DEPTH, RW_A_RANK, GROUP_W), RW_A_RANK ** -0.5),
        'rw_g_up': nrm((DEPTH, RW_G_RANK, GROUP_W), RW_G_RANK ** -0.5),
        'rw_k_k': 0.85 + nrm((DEPTH, GROUP_W), 0.05),
        'rw_k_a': 1.0 + nrm((DEPTH, GROUP_W), 0.05),
        'rw_r_k': nrm((DEPTH, N_H, HEAD), 0.1),
        'rw_gn_g': gain((DEPTH, GROUP_W)),
        'rw_gn_b': nrm((DEPTH, GROUP_W), 0.02),
        'rw_v0': nrm((dm1, GROUP_W), 0.1),
        'rw_v_down': nrm((dm1, GROUP_W, RW_V_RANK), GROUP_W ** -0.5),
        'rw_v_up': nrm((dm1, RW_V_RANK, GROUP_W), 0.1),
        'lru_conv_w': nrm((DEPTH, CONV_W, GROUP_W), CONV_W ** -0.5),
        'lru_conv_b': nrm((DEPTH, GROUP_W), 0.02),
        'lru_ga_w': nrm((DEPTH, N_H, HEAD, HEAD), HEAD ** -0.5),
        'lru_ga_b': nrm((DEPTH, GROUP_W), 0.1),
        'lru_gx_w': nrm((DEPTH, N_H, HEAD, HEAD), HEAD ** -0.5),
        'lru_gx_b': nrm((DEPTH, GROUP_W), 0.1),
        'lru_lambda': lru_lambda,
        'ml_if_b': ml_if_b,
        'ml_gn_g': gain((DEPTH, GROUP_W)),
        'ml_gn_b': nrm((DEPTH, GROUP_W), 0.02),
        'ln1_g': gain((DEPTH, D_MODEL)),
        'ln1_b': nrm((DEPTH, D_MODEL), 0.02),
        'ln2_g': gain((DEPTH, D_MODEL)),
        'ln2_b': nrm((DEPTH, D_MODEL), 0.02),
        'mlp_w1': nrm((DEPTH, D_MODEL, D_FF), D_MODEL ** -0.5),
        'mlp_w2': nrm((DEPTH, D_FF, D_MODEL), BETA * D_FF ** -0.5),
    }


def reference(x, meta, emb_ln_g, emb_ln_b, w_in, w_out, pool_w, pool_scale, rw_mu, rw_w0, rw_w_up,
              rw_a0, rw_a_up, rw_g_up, rw_k_k, rw_k_a, rw_r_k, rw_gn_g, rw_gn_b, rw_v0, rw_v_down,
              rw_v_up, lru_conv_w, lru_conv_b, lru_ga_w, lru_ga_b, lru_gx_w, lru_gx_b, lru_lambda,
              ml_if_b, ml_gn_g, ml_gn_b, ln1_g, ln1_b, ln2_g, ln2_b, mlp_w1, mlp_w2):
    B = x.shape[0]
    h = jnp.concatenate([jnp.broadcast_to(meta[None].astype(x.dtype), (B, N_META, D_MODEL)), x], axis=1)
    h = layer_norm(h, emb_ln_g, emb_ln_b)
    v_first = None
    for l in range(DEPTH):
        p = h @ w_in[l]
        p_pool, p_rw, p_lru_x, p_lru_g, p_ml = jnp.split(p, PROJ_SPLITS, axis=-1)
        y_pool = pool_mixer(p_pool, pool_w[l], pool_scale[l])
        v_mix = None if l == 0 else (rw_v0[l - 1], rw_v_down[l - 1], rw_v_up[l - 1])
        y_rw, v_first = rwkv7_mixer(p_rw, rw_mu[l], rw_w0[l], rw_w_up[l], rw_a0[l], rw_a_up[l], rw_g_up[l],
                                    rw_k_k[l], rw_k_a[l], rw_r_k[l], rw_gn_g[l], rw_gn_b[l], v_first, v_mix)
        y_lru = rglru_mixer(p_lru_x, p_lru_g, lru_conv_w[l], lru_conv_b[l], lru_ga_w[l], lru_ga_b[l],
                            lru_gx_w[l], lru_gx_b[l], lru_lambda[l])
        y_ml = mlstm_mixer(p_ml, ml_if_b[l], ml_gn_g[l], ml_gn_b[l])
        mix = jnp.concatenate([y_pool, y_rw, y_lru, y_ml], axis=-1) @ w_out[l]
        h = layer_norm(ALPHA * h + mix, ln1_g[l], ln1_b[l])
        ff = jnp.square(jax.nn.relu(h @ mlp_w1[l])) @ mlp_w2[l]
        h = layer_norm(ALPHA * h + ff, ln2_g[l], ln2_b[l])
    return h[:, N_META:]
```

```python
import numpy as np
import concourse.bass as bass
import concourse.mybir as mybir
from concourse.bass_utils import run_bass_kernel_spmd
from contextlib import ExitStack

F32 = mybir.dt.float32
BF16 = mybir.dt.bfloat16
AF = mybir.ActivationFunctionType
ALU = mybir.AluOpType
AX = mybir.AxisListType

SEM_EPOCH = 30000


class Tile:
    def __init__(self, t, name):
        self.t = t
        self.name = name
        self.w = None
        self.r = []

    def __getitem__(self, idx):
        return self.t[idx]


class Eng:
    def __init__(self, fw, name, is_pe=False):
        self.fw = fw
        self.name = name
        self.is_pe = is_pe
        self.sem = None
        self.cnt = 0
        self.waited = {}
        self.prog = []
        self.nsem = 0

    def new_sem(self):
        self.sem = self.fw.stack.enter_context(self.fw.nc.semaphore("s_%s_%d" % (self.name, self.nsem)))
        self.nsem += 1
        self.cnt = 0


class FW:
    def __init__(self, nc, stack, n_dma_sems=12):
        self.nc = nc
        self.stack = stack
        self.eng = {
            "pe": Eng(self, "pe", True),
            "dve": Eng(self, "dve"),
            "act": Eng(self, "act"),
            "pool": Eng(self, "pool"),
            "sp": Eng(self, "sp"),
        }
        for e in self.eng.values():
            e.new_sem()
        self.dma_sems = []
        for i in range(n_dma_sems):
            s = stack.enter_context(nc.semaphore("s_dma_%d" % i))
            self.dma_sems.append([s, 0])
        self.dma_rr = 0
        self.out_dma = []
        self.immediate = True
        self.ninstr = 0

    def handle(self, en):
        nc = self.nc
        return {"pe": nc.tensor, "dve": nc.vector, "act": nc.scalar, "pool": nc.gpsimd, "sp": nc.sync}[en]

    def sbuf(self, name, shape, dt=F32):
        t = self.stack.enter_context(self.nc.sbuf_tensor(name, list(shape), dt))
        return Tile(t, name)

    def psum(self, name, shape, dt=F32):
        t = self.stack.enter_context(self.nc.psum_tensor(name, list(shape), dt))
        return Tile(t, name)

    def dram(self, name, shape, dt=F32, kind=None):
        if kind is None:
            t = self.nc.dram_tensor(name, list(shape), dt)
        else:
            t = self.nc.dram_tensor(name, list(shape), dt, kind=kind)
        return Tile(t.ap(), name)

    def _deps(self, e, R, W):
        deps = []
        for t in R:
            if t.w is not None:
                deps.append(t.w)
        for t in W:
            if t.w is not None:
                deps.append(t.w)
            deps.extend(t.r)
        need = {}
        for (s, v) in deps:
            if e.is_pe and s is e.sem:
                continue
            if e.waited.get(s, 0) >= v:
                continue
            if need.get(s, 0) < v:
                need[s] = v
        for s, v in need.items():
            e.waited[s] = v
        return list(need.items())

    def _commit(self, tok, R, W):
        for t in R:
            t.r.append(tok)
        for t in W:
            t.w = tok
            t.r = []

    def op(self, en, f, R=(), W=()):
        e = self.eng[en]
        if e.cnt >= SEM_EPOCH:
            e.new_sem()
        waits = self._deps(e, R, W)
        e.cnt += 1
        sem, val = e.sem, e.cnt
        self.ninstr += 1

        def emit(h, waits=waits, f=f, sem=sem):
            for (s, v) in waits:
                h.wait_ge(s, v)
            f(h).then_inc(sem, 1)

        if self.immediate:
            emit(self.handle(en))
        else:
            e.prog.append(emit)
        tok = (sem, val)
        self._commit(tok, R, W)
        return tok

    def dma(self, en, out, in_, R=(), W=(), is_output=False, **kw):
        e = self.eng[en]
        slot = self.dma_sems[self.dma_rr]
        self.dma_rr = (self.dma_rr + 1) % len(self.dma_sems)
        s, prev = slot
        waits = self._deps(e, R, W)
        if prev > 0 and e.waited.get(s, 0) < prev:
            waits.append((s, prev))
            e.waited[s] = prev
        slot[1] = prev + 16
        val = slot[1]
        self.ninstr += 1

        def emit(h, waits=waits, s=s, out=out, in_=in_, kw=kw):
            for (ws, wv) in waits:
                h.wait_ge(ws, wv)
            h.dma_start(out=out, in_=in_, **kw).then_inc(s, 16)

        if self.immediate:
            emit(self.handle(en))
        else:
            e.prog.append(emit)
        tok = (s, val)
        self._commit(tok, R, W)
        if is_output:
            self.out_dma.append(tok)
        return tok

    def finish(self):
        e = self.eng["sp"]
        final = {}
        for (s, v) in self.out_dma:
            if final.get(s, 0) < v:
                final[s] = v
        for en in ("pe", "dve", "act", "pool"):
            ee = self.eng[en]
            if ee.cnt > 0:
                final[ee.sem] = max(final.get(ee.sem, 0), ee.cnt)
        fl = list(final.items())

        def emit(h, fl=fl):
            for (s, v) in fl:
                h.wait_ge(s, v)

        if self.immediate:
            emit(self.handle("sp"))
            return
        e.prog.append(emit)
        nc = self.nc
        with nc.Block() as block:
            @block.sync
            def _(h):
                for f in self.eng["sp"].prog:
                    f(h)

            @block.tensor
            def _(h):
                for f in self.eng["pe"].prog:
                    f(h)

            @block.vector
            def _(h):
                for f in self.eng["dve"].prog:
                    f(h)

            @block.scalar
            def _(h):
                for f in self.eng["act"].prog:
                    f(h)

            @block.gpsimd
            def _(h):
                for f in self.eng["pool"].prog:
                    f(h)


D = 1024
DFF = 4096
NMETA = 16
SEQ = 16384
DEPTH = 4
ALPHA = (2 * DEPTH) ** 0.25
LN_EPS = 1e-5
GN_EPS = 64e-5
NTOK = NMETA + SEQ // 4
TT = NMETA + SEQ


def pc_groups(layer):
    g = [("rr", 64), ("rk", 64), ("rvh", 64)]
    if layer > 0:
        g.append(("rv", 256))
    g += [("rl", 128), ("pu", 64), ("lx", 64), ("lg", 64), ("mq", 64), ("mk", 64), ("mv", 64), ("mo", 64),
          ("mi", 1), ("mf", 1)]
    off = {}
    o = 0
    for n, w in g:
        off[n] = (o, w)
        o += w
    return off, o


DBG_STOP = 99


def build_dense(first, ncol_next, blocks, ntok):
    nc = bass.Bass("TRN2", target_bir_lowering=False)
    st = ExitStack()
    with st:
        fw = FW(nc, st)
        NB = 512
        hin = fw.dram("hT", [D, ntok], F32, kind="ExternalInput")
        lnp = fw.dram("lnp", [128, 32], F32, kind="ExternalInput")
        hout = fw.dram("hT_out", [D, ntok], F32, kind="ExternalOutput")
        if not first:
            yin = fw.dram("yT", [D, ntok], F32, kind="ExternalInput")
            wo = fw.dram("w_out", [D, D], F32, kind="ExternalInput")
            w1 = fw.dram("w1", [D, DFF], F32, kind="ExternalInput")
            w2 = fw.dram("w2", [DFF, D], F32, kind="ExternalInput")
        if ncol_next:
            wi = fw.dram("w_in", [D, ncol_next], F32, kind="ExternalInput")
            pout = fw.dram("pT_out", [ncol_next, ntok], F32, kind="ExternalOutput")

        ones = fw.sbuf("ones", [128, 128], F32)
        lnp_sb = fw.sbuf("lnp_sb", [128, 32], F32)
        hT = [fw.sbuf("hTs%d" % i, [128, 8, NB], F32) for i in range(2)]
        yb = [fw.sbuf("yb%d" % i, [128, 8, NB], BF16) for i in range(2)] if not first else None
        hb = fw.sbuf("hb", [128, 8, NB], BF16)
        hid = fw.sbuf("hid", [128, 32, NB], BF16) if not first else None
        sq = [fw.sbuf("sq%d" % i, [128, NB], F32) for i in range(2)]
        mean = fw.sbuf("mean", [128, NB], F32)
        rstd = fw.sbuf("rstd", [128, NB], F32)
        tmpa = fw.sbuf("tmpa", [128, NB], F32)
        tmpb = fw.sbuf("tmpb", [128, NB], F32)
        rl = [fw.sbuf("rl%d" % i, [128, NB], F32) for i in range(2)]
        stg = [fw.sbuf("stg%d" % i, [128, NB], F32) for i in range(2)]
        NSLAB = 3
        slab = [fw.sbuf("slab%d" % i, [128, 8192], BF16) for i in range(NSLAB)]
        psm = [fw.psum("psm%d" % i, [128, NB], F32) for i in range(4)]
        ps1 = fw.psum("ps1", [128, NB], F32)
        ps2 = fw.psum("ps2", [128, NB], F32)

        fw.op("pool", lambda h: h.memset(ones[:], 1.0), W=[ones])
        fw.dma("sp", lnp_sb[:], lnp[:], R=[lnp], W=[lnp_sb])

        slabs = []
        for (t0, N) in blocks:
            if not first:
                slabs.append(("wo", wo, 0, 1024))
                for j in range(4):
                    slabs.append(("w1", w1, j * 1024, 1024))
                for j in range(4):
                    slabs.append(("w2", w2, j * 256, 256))
            if ncol_next:
                for c0 in range(0, ncol_next, 1024):
                    slabs.append(("wi", wi, c0, min(1024, ncol_next - c0)))
        state = {"issued": 0, "cur": 0}

        def issue_slab(i):
            kind, src, c0, ncols = slabs[i]
            sl = slab[i % NSLAB]
            if kind == "w2":
                dst = sl[:, :].rearrange("p (a b) -> p a b", b=256)
                sv = src.t.rearrange("(kc p) n -> p kc n", p=128)
                for q in range(4):
                    fw.dma("pool", dst[:, q * 8:(q + 1) * 8, :], sv[:, q * 8:(q + 1) * 8, c0:c0 + 256], R=[src], W=[sl])
            else:
                dst = sl[:, :].rearrange("p (a b) -> p a b", b=1024)
                sv = src.t.rearrange("(kc p) n -> p kc n", p=128)
                fw.dma("pool", dst[:, :, :ncols], sv[:, :, c0:c0 + ncols], R=[src], W=[sl])

        def next_slab():
            i = state["cur"]
            while state["issued"] < min(len(slabs), i + NSLAB):
                issue_slab(state["issued"])
                state["issued"] += 1
            state["cur"] += 1
            return slab[i % NSLAB]

        def ln_fm(t, N, gcol, bcol, eps):
            for dj in range(8):
                fw.op("pe", lambda h, dj=dj: h.matmul(ps1[:, :N], ones[:], t[:, dj, :N], start=(dj == 0), stop=(dj == 7)),
                      R=[ones, t], W=[ps1])
            for dj in range(8):
                s = sq[dj % 2]
                fw.op("act", lambda h, dj=dj, s=s: h.activation(out=s[:, :N], in_=t[:, dj, :N], func=AF.Square), R=[t], W=[s])
                fw.op("pe", lambda h, dj=dj, s=s: h.matmul(ps2[:, :N], ones[:], s[:, :N], start=(dj == 0), stop=(dj == 7)),
                      R=[ones, s], W=[ps2])
            fw.op("act", lambda h: h.activation(out=mean[:, :N], in_=ps1[:, :N], func=AF.Copy, scale=1.0 / D), R=[ps1], W=[mean])
            fw.op("dve", lambda h: h.tensor_tensor(out=tmpa[:, :N], in0=mean[:, :N], in1=mean[:, :N], op=ALU.mult), R=[mean], W=[tmpa])
            fw.op("dve", lambda h: h.scalar_tensor_tensor(out=tmpb[:, :N], in0=ps2[:, :N], scalar=1.0 / D, in1=tmpa[:, :N],
                                                          op0=ALU.mult, op1=ALU.subtract), R=[ps2, tmpa], W=[tmpb])
            fw.op("dve", lambda h: h.tensor_scalar(out=tmpb[:, :N], in0=tmpb[:, :N], scalar1=float(eps), scalar2=None, op0=ALU.add),
                  R=[tmpb], W=[tmpb])
            fw.op("act", lambda h: h.activation(out=tmpa[:, :N], in_=tmpb[:, :N], func=AF.Sqrt), R=[tmpb], W=[tmpa])
            fw.op("dve", lambda h: h.reciprocal(out=rstd[:, :N], in_=tmpa[:, :N]), R=[tmpa], W=[rstd])
            for dj in range(8):
                fw.op("dve", lambda h, dj=dj: h.tensor_tensor(out=t[:, dj, :N], in0=t[:, dj, :N], in1=mean[:, :N], op=ALU.subtract),
                      R=[t, mean], W=[t])
                fw.op("dve", lambda h, dj=dj: h.tensor_tensor(out=t[:, dj, :N], in0=t[:, dj, :N], in1=rstd[:, :N], op=ALU.mult),
                      R=[t, rstd], W=[t])
            for dj in range(8):
                g = lnp_sb[:, gcol + dj:gcol + dj + 1]
                b = lnp_sb[:, bcol + dj:bcol + dj + 1]
                fw.op("act", lambda h, dj=dj, g=g, b=b: h.activation(out=hb[:, dj, :N], in_=t[:, dj, :N], func=AF.Identity, bias=b, scale=g),
                      R=[t, lnp_sb], W=[hb])
            for dj in range(8):
                g = lnp_sb[:, gcol + dj:gcol + dj + 1]
                b = lnp_sb[:, bcol + dj:bcol + dj + 1]
                fw.op("pool", lambda h, dj=dj, g=g, b=b: h.tensor_scalar(out=t[:, dj, :N], in0=t[:, dj, :N], scalar1=g, scalar2=b,
                                                                       op0=ALU.mult, op1=ALU.add), R=[t, lnp_sb], W=[t])

        def load_block(bi):
            t0, N = blocks[bi]
            t = hT[bi % 2]
            fw.dma("sp", t[:, :, :N], hin.t.rearrange("(kc p) n -> p kc n", p=128)[:, :, t0:t0 + N], R=[hin], W=[t])
            if not first:
                y = yb[bi % 2]
                fw.dma("pool", y[:, :, :N], yin.t.rearrange("(kc p) n -> p kc n", p=128)[:, :, t0:t0 + N], R=[yin], W=[y])

        mmi = [0]

        def mm_bank():
            b = psm[mmi[0] % 4]
            mmi[0] += 1
            return b

        load_block(0)
        for bi, (t0, N) in enumerate(blocks):
            t = hT[bi % 2]
            if bi + 1 < len(blocks):
                load_block(bi + 1)
            if first:
                ln_fm(t, N, 0, 8, LN_EPS)
            elif DBG_STOP >= 2:
                y = yb[bi % 2]
                sl = next_slab()
                sv = sl[:, :].rearrange("p (a b) -> p a b", b=1024)
                for dj in range(8):
                    pb = mm_bank()
                    for kc in range(8):
                        fw.op("pe", lambda h, pb=pb, kc=kc, dj=dj, sv=sv: h.matmul(pb[:, :N], sv[:, kc, dj * 128:(dj + 1) * 128], y[:, kc, :N],
                                                                              start=(kc == 0), stop=(kc == 7)), R=[sl, y], W=[pb])
                    fw.op("dve", lambda h, pb=pb, dj=dj: h.scalar_tensor_tensor(out=t[:, dj, :N], in0=t[:, dj, :N], scalar=float(ALPHA), in1=pb[:, :N],
                                                                               op0=ALU.mult, op1=ALU.add), R=[t, pb], W=[t])
                if DBG_STOP >= 3:
                    ln_fm(t, N, 0, 8, LN_EPS)
                for j in range(4 if DBG_STOP >= 4 else 0):
                    sl = next_slab()
                    sv = sl[:, :].rearrange("p (a b) -> p a b", b=1024)
                    for fj in range(8):
                        pb = mm_bank()
                        for kc in range(8):
                            fw.op("pe", lambda h, pb=pb, kc=kc, fj=fj, sv=sv: h.matmul(pb[:, :N], sv[:, kc, fj * 128:(fj + 1) * 128], hb[:, kc, :N],
                                                                                  start=(kc == 0), stop=(kc == 7)), R=[sl, hb], W=[pb])
                        r = rl[(j * 8 + fj) % 2]
                        fw.op("act", lambda h, pb=pb, r=r: h.activation(out=r[:, :N], in_=pb[:, :N], func=AF.Relu), R=[pb], W=[r])
                        fw.op("dve", lambda h, r=r, j=j, fj=fj: h.tensor_tensor(out=hid[:, j * 8 + fj, :N], in0=r[:, :N], in1=r[:, :N], op=ALU.mult),
                              R=[r], W=[hid])
                for j in range(4 if DBG_STOP >= 5 else 0):
                    sl = next_slab()
                    sv = sl[:, :].rearrange("p (a b) -> p a b", b=256)
                    for dd in range(2):
                        dj = j * 2 + dd
                        pb = mm_bank()
                        for kc in range(32):
                            fw.op("pe", lambda h, pb=pb, kc=kc, dd=dd, sv=sv: h.matmul(pb[:, :N], sv[:, kc, dd * 128:(dd + 1) * 128], hid[:, kc, :N],
                                                                                  start=(kc == 0), stop=(kc == 31)), R=[sl, hid], W=[pb])
                        fw.op("dve", lambda h, pb=pb, dj=dj: h.scalar_tensor_tensor(out=t[:, dj, :N], in0=t[:, dj, :N], scalar=float(ALPHA), in1=pb[:, :N],
                                                                                   op0=ALU.mult, op1=ALU.add), R=[t, pb], W=[t])
                if DBG_STOP >= 6:
                    ln_fm(t, N, 16, 24, LN_EPS)
            fw.dma("sp", hout.t.rearrange("(kc p) n -> p kc n", p=128)[:, :, t0:t0 + N], t[:, :, :N], R=[t], W=[hout], is_output=True)
            if ncol_next and DBG_STOP >= 7:
                si = 0
                for c0 in range(0, ncol_next, 1024):
                    ncols = min(1024, ncol_next - c0)
                    sl = next_slab()
                    sv = sl[:, :].rearrange("p (a b) -> p a b", b=1024)
                    for m0 in range(0, ncols, 128):
                        M = min(128, ncols - m0)
                        pb = mm_bank()
                        for kc in range(8):
                            fw.op("pe", lambda h, pb=pb, kc=kc, m0=m0, M=M, sv=sv: h.matmul(pb[:M, :N], sv[:, kc, m0:m0 + M], hb[:, kc, :N],
                                                                                       start=(kc == 0), stop=(kc == 7)), R=[sl, hb], W=[pb])
                        s = stg[si % 2]
                        si += 1
                        fw.op("act", lambda h, pb=pb, s=s, M=M: h.activation(out=s[:M, :N], in_=pb[:M, :N], func=AF.Copy), R=[pb], W=[s])
                        fw.dma("sp", pout[c0 + m0:c0 + m0 + M, t0:t0 + N], s[:M, :N], R=[s], W=[pout], is_output=True)
        fw.finish()
    return nc


_PRM_NAMES = [("pw", 64), ("pscale", 1), ("psel", 4), ("pcnt", 16),
              ("cw", 4), ("cb", 1), ("gaw", 64), ("gab", 1), ("gxw", 64), ("gxb", 1), ("lam", 1),
              ("mu_r", 1), ("mu_k", 1), ("mu_vh", 1), ("mu_v", 2), ("mu_l", 1), ("w0", 1), ("a0", 1), ("k_k", 1),
              ("k_a", 1), ("r_k", 1), ("v0", 1), ("lora", 64), ("vdown", 64), ("vup", 64), ("i_b", 1), ("f_b", 1),
              ("rw_gng", 64), ("rw_gnb", 64), ("ml_gng", 64), ("ml_gnb", 64)]
PRM = {}
_o = 0
for _n, _w in _PRM_NAMES:
    PRM[_n] = (_o, _w)
    _o += _w
NPRM = _o
DECAY_C = float(np.exp(-0.5))
POOL_WINDOWS = (2, 4, 8, 16)


MIX_PARTS = {'pool', 'lru', 'rw', 'ml'}


def build_mixer(layer0, ntiles, tt):
    nc = bass.Bass("TRN2", target_bir_lowering=False)
    st = ExitStack()
    with st:
        fw = FW(nc, st)
        off, PC = pc_groups(0 if layer0 else 1)
        pT = fw.dram("pT", [PC, tt], F32, kind="ExternalInput")
        prm_d = fw.dram("prm", [128, NPRM], F32, kind="ExternalInput")
        yT = fw.dram("yT", [256, tt], F32, kind="ExternalOutput")
        if layer0:
            vf_d = fw.dram("vf_out", [64, tt], F32, kind="ExternalOutput")
        else:
            vf_d = fw.dram("vf_in", [64, tt], F32, kind="ExternalInput")
        NB = 512
        prm = fw.sbuf("prm_sb", [128, NPRM], F32)
        fw.dma("sp", prm[:], prm_d[:], R=[prm_d], W=[prm])

        def P(name, r0=0, r1=64, c0=0, c1=None):
            o, w = PRM[name]
            if c1 is None:
                c1 = w
            return prm[r0:r1, o + c0:o + c1]

        def S(name, shape):
            return fw.sbuf(name, shape, F32)

        ident = S("ident", [64, 64]); ones64 = S("ones64", [64, 64])
        mask_u8 = S("mask_u8", [64, 8, 64]); mask_su8 = S("mask_su8", [64, 8, 64]); mask_sl8 = S("mask_sl8", [64, 8, 64])
        ident8 = S("ident8", [64, 8, 64])
        G = "pool"
        fw.op(G, lambda h: h.memset(ones64[:], 1.0), W=[ones64])
        fw.op(G, lambda h: h.memset(ident[:], 0.0), W=[ident])
        fw.op(G, lambda h: h.affine_select(out=ident[:], in_=ident[:], pattern=[[-1, 64]], compare_op=ALU.not_equal, fill=1.0, base=0,
                                           channel_multiplier=1), R=[ident], W=[ident])
        fw.op(G, lambda h: h.memset(ident8[:], 0.0), W=[ident8])
        fw.op(G, lambda h: h.affine_select(out=ident8[:], in_=ident8[:], pattern=[[0, 8], [-1, 64]], compare_op=ALU.not_equal, fill=1.0, base=0,
                                           channel_multiplier=1), R=[ident8], W=[ident8])
        for m, cmp_, pat, cm in ((mask_u8, ALU.is_ge, [[0, 8], [1, 64]], -1), (mask_su8, ALU.is_gt, [[0, 8], [1, 64]], -1),
                                 (mask_sl8, ALU.is_gt, [[0, 8], [-1, 64]], 1)):
            fw.op(G, lambda h, m=m: h.memset(m[:], 1.0), W=[m])
            fw.op(G, lambda h, m=m, cmp_=cmp_, pat=pat, cm=cm: h.affine_select(out=m[:], in_=m[:], pattern=pat, compare_op=cmp_, fill=0.0, base=0,
                                                                            channel_multiplier=cm), R=[m], W=[m])
        der = S("der", [64, 8])
        fw.op("dve", lambda h: h.tensor_scalar(out=der[:, 0:1], in0=P("k_a"), scalar1=-1.0, scalar2=1.0, op0=ALU.mult, op1=ALU.add), R=[prm], W=[der])
        fw.op("act", lambda h: h.activation(out=der[:, 3:4], in_=P("lam"), func=AF.Exp, scale=-1.0), R=[prm], W=[der])
        fw.op("dve", lambda h: h.tensor_scalar(out=der[:, 3:4], in0=der[:, 3:4], scalar1=1.0, scalar2=None, op0=ALU.add), R=[der], W=[der])
        fw.op("act", lambda h: h.activation(out=der[:, 3:4], in_=der[:, 3:4], func=AF.Ln), R=[der], W=[der])
        fw.op("dve", lambda h: h.tensor_scalar(out=der[:, 1:2], in0=der[:, 3:4], scalar1=-8.0, scalar2=None, op0=ALU.mult), R=[der], W=[der])
        fw.op("dve", lambda h: h.tensor_scalar(out=der[:, 2:3], in0=der[:, 3:4], scalar1=-16.0, scalar2=None, op0=ALU.mult), R=[der], W=[der])
        fw.op("dve", lambda h: h.tensor_scalar(out=der[:, 4:5], in0=P("f_b"), scalar1=-1.0, scalar2=None, op0=ALU.mult), R=[prm], W=[der])

        def inbuf(name, shape):
            return [S("%s%d" % (name, i), shape) for i in range(2)]
        i_rr = inbuf("i_rr", [64, 1 + NB]); i_rk = inbuf("i_rk", [64, 1 + NB]); i_rvh = inbuf("i_rvh", [64, 1 + NB])
        i_rl = inbuf("i_rl", [128, 1 + NB])
        i_rv = [S("i_rv", [128, 2, 1 + NB])] * 2 if not layer0 else None
        i_vf = [S("i_vf", [64, NB])] * 2 if not layer0 else None
        i_pu = inbuf("i_pu", [64, 15 + NB]); i_lx = inbuf("i_lx", [64, 3 + NB]); i_lg = inbuf("i_lg", [64, NB])
        i_mq = inbuf("i_mq", [64, NB]); i_mk = inbuf("i_mk", [64, NB]); i_mv = inbuf("i_mv", [64, NB]); i_mo = inbuf("i_mo", [64, NB])
        i_mi = inbuf("i_mi", [64, NB]); i_mf = inbuf("i_mf", [64, NB])

        tiles = [(0, 16, 16, 1)] + [(16 + NB * i, NB, 64, 8) for i in range(ntiles)]

        def load_tile(ti):
            t0, N, L, nch = tiles[ti]
            par = ti % 2

            def ld(buf, grp, halo, rows=None):
                o, w = off[grp]
                b = buf[par]
                if t0 == 0:
                    if halo:
                        fw.op("pool", lambda h: h.memset(b[:, 0:halo], 0.0), W=[b])
                    fw.dma("sp", b[:, halo:halo + N], pT[o:o + w, 0:N], R=[pT], W=[b])
                else:
                    fw.dma("sp", b[:, 0:halo + N], pT[o:o + w, t0 - halo:t0 + N], R=[pT], W=[b])
            ld(i_rr, "rr", 1); ld(i_rk, "rk", 1); ld(i_rvh, "rvh", 1); ld(i_rl, "rl", 1)
            ld(i_pu, "pu", 15); ld(i_lx, "lx", 3); ld(i_lg, "lg", 0)
            ld(i_mq, "mq", 0); ld(i_mk, "mk", 0); ld(i_mv, "mv", 0); ld(i_mo, "mo", 0)
            for buf, grp in ((i_mi, "mi"), (i_mf, "mf")):
                o, w = off[grp]
                fw.dma("sp", buf[par][:, 0:N], pT[o, t0:t0 + N].partition_broadcast(64), R=[pT], W=[buf[par]])

        def load_tile_late(ti):
            t0, N, L, nch = tiles[ti]
            o, w = off["rv"]
            b = i_rv[0]
            for j in range(2):
                if t0 == 0:
                    fw.op("pool", lambda h, j=j: h.memset(b[:, j, 0:1], 0.0), W=[b])
                    fw.dma("sp", b[:, j, 1:1 + N], pT[o + 128 * j:o + 128 * (j + 1), 0:N], R=[pT], W=[b])
                else:
                    fw.dma("sp", b[:, j, 0:1 + N], pT[o + 128 * j:o + 128 * (j + 1), t0 - 1:t0 + N], R=[pT], W=[b])
            fw.dma("sp", i_vf[0][:, :N], vf_d[:, t0:t0 + N], R=[vf_d], W=[i_vf[0]])

        bank = [fw.psum("bk%d" % i, [128, 512], F32) for i in range(8)]
        bki = [0]

        def nb():
            b = bank[bki[0] % 6]
            bki[0] += 1
            return b
        bk_seq = bank[6]
        bk_y = bank[7]

        def b3(b, w=64):
            return b[:, 0:8 * w].rearrange("p (a b) -> p a b", b=w)

        T2 = lambda n: S(n, [64, NB])
        T3 = lambda n: S(n, [64, 8, 64])
        s2 = S("s2", [64, 15 + NB]); s4 = S("s4", [64, 15 + NB]); s8 = s2; s16 = s4
        pacc = T2("pacc"); o_pool = T2("o_pool")
        xc = T2("xc"); lr = T2("lr"); li = T2("li"); la = T2("la"); la2 = T2("la2"); lu = li; lhs_ = T2("lhs_"); lgl = T2("lgl")
        o_lru = lgl; lru_h = S("lru_h", [64, 1])
        fw.op("pool", lambda h: h.memset(lru_h[:], 0.0), W=[lru_h])
        tmp = S("tmp", [128, NB]); pm_r = T2("pm_r"); pm_k = T2("pm_k"); pm_vh = T2("pm_vh"); pm_l = S("pm_l", [128, NB])
        pm_v = S("pm_v", [128, 2, NB]) if not layer0 else None
        dsig = T2("dsig"); ra = T2("ra"); vd = S("vd", [32, NB]); gate = T2("gate")
        kk = T2("kk"); sc1 = T2("sc1"); sc2 = T2("sc2"); kap = kk; kp = T2("kp"); bv = T2("bv"); rkr = T2("rkr")
        cs = T2("cs"); csx = sc1; pin = T2("pin"); pex = T2("pex"); pinv = T2("pinv")
        rt = T2("rt"); kat = T2("kat"); kt = T2("kt"); bt = T2("bt")
        g_tok = T3("g_tok"); bsum = S("bsum", [64, 8, 2]); vtok = T3("vtok"); ktok = T3("ktok"); btok = T3("btok")
        X = [T3("X0"), T3("X1")]; XT = [T3("XT0"), T3("XT1")]; Q = [T3("Q0"), T3("Q1")]
        AkT = T3("AkT"); CkT = T3("CkT"); CbT = T3("CbT")
        Hs = [S("H0", [64, 64]), S("H1", [64, 64])]
        rhs_sb = [S("rhs0", [64, 64]), S("rhs1", [64, 64])]; nU = [S("nU0", [64, 64]), S("nU1", [64, 64])]
        Ysb = T3("Ysb"); sqY = T3("sqY"); gst = S("gst", [64, 8, 8]); o_rw = T2("o_rw")
        fw.op("pool", lambda h: h.memset(Hs[0][:], 0.0), W=[Hs[0]])
        mrow = [sc1, sc2, cs, pex, pinv]
        kq = T2("kq"); qq = T2("qq"); gam = S("gam", [64, 8])
        vaug = S("vaug", [64, 8, 66]); ktok_m = T3("ktok_m"); otok = T3("otok"); STm = T3("STm")
        Cm = [S("Cm0", [64, 66]), S("Cm1", [64, 66])]
        nd = S("nd", [64, 8, 66]); hh = T3("hh"); o_ml = T2("o_ml")
        fw.op("pool", lambda h: h.memset(Cm[0][:], 0.0), W=[Cm[0]])
        fw.op("pool", lambda h: h.memset(vaug[:], 1.0), W=[vaug])
        hstate = {"H": 0, "C": 0}

        def group_norm(Y, L, nch, gname, bname):
            fw.op("dve", lambda h: h.tensor_reduce(out=gst[:L, :nch, 0], in_=Y[:L, :nch, :], axis=AX.X, op=ALU.add), R=[Y], W=[gst])
            fw.op("act", lambda h: h.activation(out=sqY[:L, :nch, :], in_=Y[:L, :nch, :], func=AF.Square), R=[Y], W=[sqY])
            fw.op("dve", lambda h: h.tensor_reduce(out=gst[:L, :nch, 1], in_=sqY[:L, :nch, :], axis=AX.X, op=ALU.add), R=[sqY], W=[gst])
            fw.op("dve", lambda h: h.tensor_scalar(out=gst[:L, :nch, 2], in0=gst[:L, :nch, 0], scalar1=1.0 / 64, scalar2=None, op0=ALU.mult), R=[gst], W=[gst])
            fw.op("dve", lambda h: h.tensor_tensor(out=gst[:L, :nch, 3], in0=gst[:L, :nch, 2], in1=gst[:L, :nch, 2], op=ALU.mult), R=[gst], W=[gst])
            fw.op("dve", lambda h: h.scalar_tensor_tensor(out=gst[:L, :nch, 4], in0=gst[:L, :nch, 1], scalar=1.0 / 64, in1=gst[:L, :nch, 3],
                                                          op0=ALU.mult, op1=ALU.subtract), R=[gst], W=[gst])
            fw.op("dve", lambda h: h.tensor_scalar(out=gst[:L, :nch, 4], in0=gst[:L, :nch, 4], scalar1=float(GN_EPS), scalar2=None, op0=ALU.add), R=[gst], W=[gst])
            fw.op("act", lambda h: h.activation(out=gst[:L, :nch, 5], in_=gst[:L, :nch, 4], func=AF.Sqrt), R=[gst], W=[gst])
            fw.op("dve", lambda h: h.reciprocal(out=gst[:L, :nch, 6], in_=gst[:L, :nch, 5]), R=[gst], W=[gst])
            for j in range(nch):
                fw.op("dve", lambda h, j=j: h.tensor_scalar(out=Y[:L, j, :], in0=Y[:L, j, :], scalar1=gst[:L, j, 2:3], scalar2=gst[:L, j, 6:7],
                                                            op0=ALU.subtract, op1=ALU.mult), R=[Y, gst], W=[Y])
            go, _ = PRM[gname]
            bo, _ = PRM[bname]
            for j in range(nch):
                fw.op("dve", lambda h, j=j: h.tensor_tensor(out=Y[:L, j, :], in0=Y[:L, j, :], in1=prm[0:L, go:go + 64], op=ALU.mult), R=[Y, prm], W=[Y])
                fw.op("dve", lambda h, j=j: h.tensor_tensor(out=Y[:L, j, :], in0=Y[:L, j, :], in1=prm[0:L, bo:bo + 64], op=ALU.add), R=[Y, prm], W=[Y])

        def to_channel_major(Y, L, nch, N, otile, row0, t0):
            pb = nb()
            for j in range(nch):
                fw.op("pe", lambda h, j=j: h.transpose(pb[0:64, j * L:(j + 1) * L], Y[:L, j, :], ident[:L, :L]), R=[Y, ident], W=[pb])
            fw.op("act", lambda h: h.activation(out=otile[:, :N], in_=pb[0:64, :N], func=AF.Copy), R=[pb], W=[otile])
            fw.dma("sp", yT[row0:row0 + 64, t0:t0 + N], otile[:, :N], R=[otile], W=[yT], is_output=True)

        load_tile(0)
        for ti, (t0, N, L, nch) in enumerate(tiles):
            par = ti % 2
            if ti + 1 < len(tiles):
                load_tile(ti + 1)
            if not layer0:
                load_tile_late(ti)
            ch = lambda j: slice(j * L, (j + 1) * L)

            if 'pool' in MIX_PARTS:
                u = i_pu[par]
                Hn = 15 + N
                fw.op("pool", lambda h: h.tensor_tensor(out=s2[:, 1:Hn], in0=u[:, 1:Hn], in1=u[:, 0:Hn - 1], op=ALU.add), R=[u], W=[s2])
                fw.op("dve", lambda h: h.tensor_scalar(out=pacc[:, :N], in0=s2[:, 15:Hn], scalar1=P("psel", c0=0, c1=1), scalar2=None, op0=ALU.mult),
                      R=[s2, prm], W=[pacc])
                fw.op("pool", lambda h: h.tensor_tensor(out=s4[:, 3:Hn], in0=s2[:, 3:Hn], in1=s2[:, 1:Hn - 2], op=ALU.add), R=[s2], W=[s4])
                fw.op("dve", lambda h: h.scalar_tensor_tensor(out=pacc[:, :N], in0=s4[:, 15:Hn], scalar=P("psel", c0=1, c1=2),
                                                              in1=pacc[:, :N], op0=ALU.mult, op1=ALU.add), R=[s4, pacc, prm], W=[pacc])
                fw.op("pool", lambda h: h.tensor_tensor(out=s8[:, 7:Hn], in0=s4[:, 7:Hn], in1=s4[:, 3:Hn - 4], op=ALU.add), R=[s4, s2], W=[s8])
                fw.op("dve", lambda h: h.scalar_tensor_tensor(out=pacc[:, :N], in0=s8[:, 15:Hn], scalar=P("psel", c0=2, c1=3),
                                                              in1=pacc[:, :N], op0=ALU.mult, op1=ALU.add), R=[s8, pacc, prm], W=[pacc])
                fw.op("pool", lambda h: h.tensor_tensor(out=s16[:, 15:Hn], in0=s8[:, 15:Hn], in1=s8[:, 7:Hn - 8], op=ALU.add), R=[s8, s4], W=[s16])
                fw.op("dve", lambda h: h.scalar_tensor_tensor(out=pacc[:, :N], in0=s16[:, 15:Hn], scalar=P("psel", c0=3, c1=4),
                                                              in1=pacc[:, :N], op0=ALU.mult, op1=ALU.add), R=[s16, pacc, prm], W=[pacc])
                if t0 == 0:
                    fw.op("dve", lambda h: h.tensor_tensor(out=pacc[:, :N], in0=pacc[:, :N], in1=P("pcnt", c0=0, c1=N), op=ALU.mult), R=[pacc, prm], W=[pacc])
                fw.op("dve", lambda h: h.tensor_tensor(out=pacc[:, :N], in0=pacc[:, :N], in1=u[:, 15:Hn], op=ALU.subtract), R=[pacc, u], W=[pacc])
                pb = nb()
                fw.op("pe", lambda h: h.matmul(pb[0:64, :N], P("pw"), pacc[:, :N], start=True, stop=True), R=[prm, pacc], W=[pb])
                fw.op("act", lambda h: h.activation(out=o_pool[:, :N], in_=pb[0:64, :N], func=AF.Identity, scale=P("pscale")), R=[pb, prm], W=[o_pool])
                fw.dma("sp", yT[0:64, t0:t0 + N], o_pool[:, :N], R=[o_pool], W=[yT], is_output=True)

            if 'lru' in MIX_PARTS:
                x = i_lx[par]
                fw.op("dve", lambda h: h.tensor_scalar(out=xc[:, :N], in0=x[:, 3:3 + N], scalar1=P("cw", c0=3, c1=4), scalar2=P("cb"), op0=ALU.mult, op1=ALU.add),
                      R=[x, prm], W=[xc])
                for jj in range(3):
                    fw.op("dve", lambda h, jj=jj: h.scalar_tensor_tensor(out=xc[:, :N], in0=x[:, jj:jj + N], scalar=P("cw", c0=jj, c1=jj + 1), in1=xc[:, :N],
                                                                        op0=ALU.mult, op1=ALU.add), R=[x, xc, prm], W=[xc])
                pb = nb()
                fw.op("pe", lambda h: h.matmul(pb[0:64, :N], P("gaw"), xc[:, :N], start=True, stop=True), R=[prm, xc], W=[pb])
                fw.op("act", lambda h: h.activation(out=lr[:, :N], in_=pb[0:64, :N], func=AF.Sigmoid, bias=P("gab")), R=[pb, prm], W=[lr])
                pb = nb()
                fw.op("pe", lambda h: h.matmul(pb[0:64, :N], P("gxw"), xc[:, :N], start=True, stop=True), R=[prm, xc], W=[pb])
                fw.op("act", lambda h: h.activation(out=li[:, :N], in_=pb[0:64, :N], func=AF.Sigmoid, bias=P("gxb")), R=[pb, prm], W=[li])
                fw.op("act", lambda h: h.activation(out=la[:, :N], in_=lr[:, :N], func=AF.Exp, scale=der[:, 1:2]), R=[lr, der], W=[la])
                fw.op("act", lambda h: h.activation(out=la2[:, :N], in_=lr[:, :N], func=AF.Exp, scale=der[:, 2:3]), R=[lr, der], W=[la2])
                fw.op("dve", lambda h: h.tensor_scalar(out=la2[:, :N], in0=la2[:, :N], scalar1=-1.0, scalar2=1.0, op0=ALU.mult, op1=ALU.add), R=[la2], W=[la2])
                fw.op("act", lambda h: h.activation(out=la2[:, :N], in_=la2[:, :N], func=AF.Sqrt), R=[la2], W=[la2])
                fw.op("dve", lambda h: h.tensor_tensor(out=lu[:, :N], in0=li[:, :N], in1=xc[:, :N], op=ALU.mult), R=[li, xc], W=[lu])
                fw.op("dve", lambda h: h.tensor_tensor(out=lu[:, :N], in0=lu[:, :N], in1=la2[:, :N], op=ALU.mult), R=[lu, la2], W=[lu])
                fw.op("dve", lambda h: h.scalar_tensor_tensor(out=lu[:, 0:1], in0=la[:, 0:1], scalar=lru_h[:, 0:1], in1=lu[:, 0:1], op0=ALU.mult, op1=ALU.add),
                      R=[la, lu, lru_h], W=[lu])
                fw.op("dve", lambda h: h.tensor_tensor_scan(out=lhs_[:, :N], data0=la[:, :N], data1=lu[:, :N], initial=0.0, op0=ALU.mult, op1=ALU.add),
                      R=[la, lu], W=[lhs_])
                fw.op("dve", lambda h: h.tensor_copy(lru_h[:, 0:1], lhs_[:, N - 1:N]), R=[lhs_], W=[lru_h])
                fw.op("act", lambda h: h.activation(out=lgl[:, :N], in_=i_lg[par][:, :N], func=AF.Gelu_apprx_tanh), R=[i_lg[par]], W=[lgl])
                fw.op("dve", lambda h: h.tensor_tensor(out=o_lru[:, :N], in0=lhs_[:, :N], in1=lgl[:, :N], op=ALU.mult), R=[lhs_, lgl], W=[o_lru])
                fw.dma("sp", yT[128:192, t0:t0 + N], o_lru[:, :N], R=[o_lru], W=[yT], is_output=True)

            if 'rw' in MIX_PARTS:
                def lerp(src, dst, mu, rows):
                    fw.op("dve", lambda h: h.tensor_tensor(out=tmp[0:rows, :N], in0=src[:, 0:N], in1=src[:, 1:N + 1], op=ALU.subtract), R=[], W=[tmp])
                    fw.op("dve", lambda h: h.scalar_tensor_tensor(out=dst, in0=tmp[0:rows, :N], scalar=mu, in1=src[:, 1:N + 1], op0=ALU.mult, op1=ALU.add),
                          R=[tmp, prm], W=[])
                def lerp_t(srct, srcap, dstt, dstap, mu, rows):
                    fw.op("dve", lambda h: h.tensor_tensor(out=tmp[0:rows, :N], in0=srcap[:, 0:N], in1=srcap[:, 1:N + 1], op=ALU.subtract), R=[srct], W=[tmp])
                    fw.op("dve", lambda h: h.scalar_tensor_tensor(out=dstap, in0=tmp[0:rows, :N], scalar=mu, in1=srcap[:, 1:N + 1], op0=ALU.mult, op1=ALU.add),
                          R=[tmp, prm, srct], W=[dstt])
                lerp_t(i_rr[par], i_rr[par][:, :], pm_r, pm_r[:, :N], P("mu_r"), 64)
                lerp_t(i_rk[par], i_rk[par][:, :], pm_k, pm_k[:, :N], P("mu_k"), 64)
                lerp_t(i_rvh[par], i_rvh[par][:, :], pm_vh, pm_vh[:, :N], P("mu_vh"), 64)
                lerp_t(i_rl[par], i_rl[par][:, :], pm_l, pm_l[:, :N], P("mu_l", 0, 128), 128)
                if not layer0:
                    for j in range(2):
                        lerp_t(i_rv[par], i_rv[par][:, j, :], pm_v, pm_v[:, j, :N], P("mu_v", 0, 128, j, j + 1), 128)
                fw.op("act", lambda h: h.activation(out=pm_l[0:32, :N], in_=pm_l[0:32, :N], func=AF.Tanh), R=[pm_l], W=[pm_l])
                fw.op("act", lambda h: h.activation(out=pm_l[64:128, :N], in_=pm_l[64:128, :N], func=AF.Sigmoid), R=[pm_l], W=[pm_l])
                pb = nb()
                fw.op("pe", lambda h: h.matmul(pb[0:64, :N], P("lora", 0, 32), pm_l[0:32, :N], start=True, stop=True), R=[prm, pm_l], W=[pb])
                fw.op("act", lambda h: h.activation(out=dsig[:, :N], in_=pb[0:64, :N], func=AF.Sigmoid, bias=P("w0")), R=[pb, prm], W=[dsig])
                pb = nb()
                fw.op("pe", lambda h: h.matmul(pb[0:64, :N], P("lora", 32, 64), pm_l[32:64, :N], start=True, stop=True), R=[prm, pm_l], W=[pb])
                fw.op("act", lambda h: h.activation(out=ra[:, :N], in_=pb[0:64, :N], func=AF.Sigmoid, bias=P("a0")), R=[pb, prm], W=[ra])
                pb = nb()
                pb3 = b3(pb)
                for j in range(nch):
                    fw.op("pe", lambda h, j=j: h.matmul(pb3[0:L, j, :], pm_l[64:128, ch(j)], P("lora", 64, 128), start=True, stop=True), R=[prm, pm_l], W=[pb])
                fw.op("act", lambda h: h.activation(out=g_tok[:L, :nch, :], in_=pb3[0:L, :nch, :], func=AF.Copy), R=[pb], W=[g_tok])
                if layer0:
                    fw.dma("sp", vf_d[:, t0:t0 + N], pm_vh[:, :N], R=[pm_vh], W=[vf_d], is_output=True)
                else:
                    pb = nb()
                    for j in range(2):
                        fw.op("pe", lambda h, j=j: h.matmul(pb[0:32, :N], P("vdown", 0, 128, j * 32, (j + 1) * 32), pm_v[:, j, :N], start=(j == 0), stop=(j == 1)),
                              R=[prm, pm_v], W=[pb])
                    fw.op("act", lambda h: h.activation(out=vd[:, :N], in_=pb[0:32, :N], func=AF.Copy), R=[pb], W=[vd])
                    pb = nb()
                    fw.op("pe", lambda h: h.matmul(pb[0:64, :N], P("vup", 0, 32), vd[:, :N], start=True, stop=True), R=[prm, vd], W=[pb])
                    fw.op("act", lambda h: h.activation(out=gate[:, :N], in_=pb[0:64, :N], func=AF.Sigmoid, bias=P("v0")), R=[pb, prm], W=[gate])
                    vf = i_vf[par]
                    fw.op("dve", lambda h: h.tensor_tensor(out=sc1[:, :N], in0=vf[:, :N], in1=pm_vh[:, :N], op=ALU.subtract), R=[vf, pm_vh], W=[sc1])
                    fw.op("dve", lambda h: h.tensor_tensor(out=sc1[:, :N], in0=sc1[:, :N], in1=gate[:, :N], op=ALU.mult), R=[sc1, gate], W=[sc1])
                    fw.op("dve", lambda h: h.tensor_tensor(out=pm_vh[:, :N], in0=pm_vh[:, :N], in1=sc1[:, :N], op=ALU.add), R=[sc1, pm_vh], W=[pm_vh])
                fw.op("dve", lambda h: h.tensor_scalar(out=kk[:, :N], in0=pm_k[:, :N], scalar1=P("k_k"), scalar2=None, op0=ALU.mult), R=[pm_k, prm], W=[kk])
                fw.op("act", lambda h: h.activation(out=sc1[:, :N], in_=kk[:, :N], func=AF.Square), R=[kk], W=[sc1])
                pb = nb()
                fw.op("pe", lambda h: h.matmul(pb[0:64, :N], ones64[:], sc1[:, :N], start=True, stop=True), R=[ones64, sc1], W=[pb])
                fw.op("dve", lambda h: h.tensor_scalar(out=sc2[:, :N], in0=pb[0:64, :N], scalar1=1e-12, scalar2=None, op0=ALU.add), R=[pb], W=[sc2])
                fw.op("act", lambda h: h.activation(out=sc2[:, :N], in_=sc2[:, :N], func=AF.Sqrt), R=[sc2], W=[sc2])
                fw.op("dve", lambda h: h.reciprocal(out=sc2[:, :N], in_=sc2[:, :N]), R=[sc2], W=[sc2])
                fw.op("dve", lambda h: h.tensor_tensor(out=kap[:, :N], in0=kk[:, :N], in1=sc2[:, :N], op=ALU.mult), R=[kk, sc2], W=[kap])
                fw.op("dve", lambda h: h.tensor_scalar(out=sc1[:, :N], in0=ra[:, :N], scalar1=P("k_a"), scalar2=der[:, 0:1], op0=ALU.mult, op1=ALU.add),
                      R=[ra, prm, der], W=[sc1])
                fw.op("dve", lambda h: h.tensor_tensor(out=kp[:, :N], in0=pm_k[:, :N], in1=sc1[:, :N], op=ALU.mult), R=[pm_k, sc1], W=[kp])
                fw.op("dve", lambda h: h.tensor_tensor(out=bv[:, :N], in0=kap[:, :N], in1=ra[:, :N], op=ALU.mult), R=[kap, ra], W=[bv])
                fw.op("dve", lambda h: h.scalar_tensor_tensor(out=rkr[:, :N], in0=pm_r[:, :N], scalar=P("r_k"), in1=kp[:, :N], op0=ALU.mult, op1=ALU.mult),
                      R=[pm_r, kp, prm], W=[rkr])
                pb = nb()
                pbs = b3(pb, 2)
                for j in range(nch):
                    fw.op("pe", lambda h, j=j: h.matmul(pbs[0:L, j, :], rkr[:, ch(j)], ones64[:, 0:2], start=True, stop=True), R=[rkr, ones64], W=[pb])
                fw.op("act", lambda h: h.activation(out=bsum[:L, :nch, :], in_=pbs[0:L, :nch, :], func=AF.Copy), R=[pb], W=[bsum])
                for j in range(nch):
                    fw.op("dve", lambda h, j=j: h.tensor_tensor_scan(out=cs[:, ch(j)], data0=ones64[:, 0:L], data1=dsig[:, ch(j)], initial=0.0,
                                                                    op0=ALU.mult, op1=ALU.add), R=[ones64, dsig], W=[cs])
                fw.op("dve", lambda h: h.tensor_tensor(out=csx[:, :N], in0=cs[:, :N], in1=dsig[:, :N], op=ALU.subtract), R=[cs, dsig], W=[csx])
                fw.op("act", lambda h: h.activation(out=pin[:, :N], in_=cs[:, :N], func=AF.Exp, scale=-DECAY_C), R=[cs], W=[pin])
                fw.op("act", lambda h: h.activation(out=pex[:, :N], in_=csx[:, :N], func=AF.Exp, scale=-DECAY_C), R=[csx], W=[pex])
                fw.op("act", lambda h: h.activation(out=pinv[:, :N], in_=cs[:, :N], func=AF.Exp, scale=DECAY_C), R=[cs], W=[pinv])
                fw.op("dve", lambda h: h.tensor_tensor(out=rt[:, :N], in0=pm_r[:, :N], in1=pin[:, :N], op=ALU.mult), R=[pm_r, pin], W=[rt])
                fw.op("dve", lambda h: h.tensor_tensor(out=kat[:, :N], in0=kap[:, :N], in1=pex[:, :N], op=ALU.mult), R=[kap, pex], W=[kat])
                fw.op("dve", lambda h: h.tensor_tensor(out=kt[:, :N], in0=kp[:, :N], in1=pinv[:, :N], op=ALU.mult), R=[kp, pinv], W=[kt])
                fw.op("dve", lambda h: h.tensor_tensor(out=bt[:, :N], in0=bv[:, :N], in1=pinv[:, :N], op=ALU.mult), R=[bv, pinv], W=[bt])
                for srct, dstt, eng in ((pm_vh, vtok, "act"), (kt, ktok, "dve"), (bt, btok, "act")):
                    pb = nb()
                    pb3 = b3(pb)
                    for j in range(nch):
                        fw.op("pe", lambda h, j=j: h.transpose(pb3[0:L, j, :], srct[:, ch(j)], ident[:, :]), R=[srct, ident], W=[pb])
                    if eng == "act":
                        fw.op("act", lambda h: h.activation(out=dstt[:L, :nch, :], in_=pb3[0:L, :nch, :], func=AF.Copy), R=[pb], W=[dstt])
                    else:
                        fw.op("dve", lambda h: h.tensor_copy(dstt[:L, :nch, :], pb3[0:L, :nch, :]), R=[pb], W=[dstt])
                def gram(lt, rt_, dst, mask, neg):
                    pb = nb()
                    pb3 = b3(pb)
                    for j in range(nch):
                        fw.op("pe", lambda h, j=j: h.matmul(pb3[0:L, j, 0:L], lt[:, ch(j)], rt_[:, ch(j)], start=True, stop=True), R=[lt, rt_], W=[pb])
                    fw.op("dve", lambda h: h.scalar_tensor_tensor(out=dst[:L, :nch, :L], in0=pb3[0:L, :nch, 0:L], scalar=(-1.0 if neg else 1.0),
                                                                  in1=mask[:L, :nch, :L], op0=ALU.mult, op1=ALU.mult), R=[pb, mask], W=[dst])
                gram(bt, kat, X[0], mask_su8, True)
                gram(kat, bt, XT[0], mask_sl8, True)
                gram(kt, kat, AkT, mask_su8, False)
                gram(kt, rt, CkT, mask_u8, False)
                gram(bt, rt, CbT, mask_u8, False)
                fw.op("dve", lambda h: h.tensor_tensor(out=Q[0][:L, :nch, :L], in0=X[0][:L, :nch, :L], in1=ident8[:L, :nch, :L], op=ALU.add),
                      R=[X[0], ident8], W=[Q[0]])
                nlev = {64: 5, 16: 3}[L]
                xi = 0
                qi = 0
                for lev in range(nlev):
                    last = lev == nlev - 1
                    Xc, XTc, Xn, XTn = X[xi], XT[xi], X[1 - xi], XT[1 - xi]
                    pb = nb()
                    pb3 = b3(pb)
                    for j in range(nch):
                        fw.op("pe", lambda h, j=j: h.matmul(pb3[0:L, j, 0:L], Xc[:L, j, :L], XTc[:L, j, :L], start=True, stop=True), R=[Xc, XTc], W=[pb])
                    fw.op("act", lambda h: h.activation(out=XTn[:L, :nch, :L], in_=pb3[0:L, :nch, 0:L], func=AF.Copy), R=[pb], W=[XTn])
                    if not last:
                        pb = nb()
                        pb3 = b3(pb)
                        for j in range(nch):
                            fw.op("pe", lambda h, j=j: h.matmul(pb3[0:L, j, 0:L], XTc[:L, j, :L], Xc[:L, j, :L], start=True, stop=True), R=[Xc, XTc], W=[pb])
                        fw.op("dve", lambda h: h.tensor_copy(Xn[:L, :nch, :L], pb3[0:L, :nch, 0:L]), R=[pb], W=[Xn])
                    pb = nb()
                    pb3 = b3(pb)
                    Qc, Qn = Q[qi], Q[1 - qi]
                    for j in range(nch):
                        fw.op("pe", lambda h, j=j: h.matmul(pb3[0:L, j, 0:L], XTn[:L, j, :L], Qc[:L, j, :L], start=True, stop=True), R=[XTn, Qc], W=[pb])
                    fw.op("dve", lambda h: h.tensor_tensor(out=Qn[:L, :nch, :L], in0=pb3[0:L, :nch, 0:L], in1=Qc[:L, :nch, :L], op=ALU.add), R=[pb, Qc], W=[Qn])
                    xi = 1 - xi
                    qi = 1 - qi
                Qf = Q[qi]
                by3 = b3(bk_y)
                for j in range(nch):
                    Hc, Hn_ = Hs[hstate["H"]], Hs[1 - hstate["H"]]
                    rs, nu = rhs_sb[j % 2], nU[j % 2]
                    fw.op("pe", lambda h: h.matmul(bk_seq[0:L, 0:64], kat[:, ch(j)], Hc[:, :], start=True, stop=False), R=[kat, Hc], W=[bk_seq])
                    fw.op("pe", lambda h: h.matmul(bk_seq[0:L, 0:64], AkT[:L, j, :L], vtok[:L, j, :], start=False, stop=True), R=[AkT, vtok], W=[bk_seq])
                    fw.op("act", lambda h: h.activation(out=rs[:L, :], in_=bk_seq[0:L, 0:64], func=AF.Copy), R=[bk_seq], W=[rs])
                    fw.op("pe", lambda h: h.matmul(bk_seq[0:L, 64:128], Qf[:L, j, :L], rs[:L, :], start=True, stop=True), R=[Qf, rs], W=[bk_seq])
                    fw.op("act", lambda h: h.activation(out=nu[:L, :], in_=bk_seq[0:L, 64:128], func=AF.Copy, scale=-1.0), R=[bk_seq], W=[nu])
                    fw.op("pe", lambda h: h.matmul(by3[0:L, j, :], rt[:, ch(j)], Hc[:, :], start=True, stop=False), R=[rt, Hc], W=[bk_y])
                    fw.op("pe", lambda h: h.matmul(by3[0:L, j, :], CkT[:L, j, :L], vtok[:L, j, :], start=False, stop=False), R=[CkT, vtok], W=[bk_y])
                    fw.op("pe", lambda h: h.matmul(by3[0:L, j, :], CbT[:L, j, :L], nu[:L, :], start=False, stop=True), R=[CbT, nu], W=[bk_y])
                    fw.op("pe", lambda h: h.matmul(bk_seq[0:64, 128:192], ident[:, :], Hc[:, :], start=True, stop=False), R=[ident, Hc], W=[bk_seq])
                    fw.op("pe", lambda h: h.matmul(bk_seq[0:64, 128:192], ktok[:L, j, :], vtok[:L, j, :], start=False, stop=False), R=[ktok, vtok], W=[bk_seq])
                    fw.op("pe", lambda h: h.matmul(bk_seq[0:64, 128:192], btok[:L, j, :], nu[:L, :], start=False, stop=True), R=[btok, nu], W=[bk_seq])
                    ce = (j + 1) * L - 1
                    fw.op("dve", lambda h: h.tensor_scalar(out=Hn_[:, :], in0=bk_seq[0:64, 128:192], scalar1=pin[:, ce:ce + 1], scalar2=None, op0=ALU.mult),
                          R=[bk_seq, pin], W=[Hn_])
                    hstate["H"] = 1 - hstate["H"]
                fw.op("act", lambda h: h.activation(out=Ysb[:L, :nch, :], in_=by3[0:L, :nch, :], func=AF.Copy), R=[bk_y], W=[Ysb])
                group_norm(Ysb, L, nch, "rw_gng", "rw_gnb")
                for j in range(nch):
                    fw.op("dve", lambda h, j=j: h.scalar_tensor_tensor(out=Ysb[:L, j, :], in0=vtok[:L, j, :], scalar=bsum[:L, j, 0:1], in1=Ysb[:L, j, :],
                                                                      op0=ALU.mult, op1=ALU.add), R=[vtok, bsum, Ysb], W=[Ysb])
                fw.op("dve", lambda h: h.tensor_tensor(out=Ysb[:L, :nch, :], in0=Ysb[:L, :nch, :], in1=g_tok[:L, :nch, :], op=ALU.mult), R=[Ysb, g_tok], W=[Ysb])
                to_channel_major(Ysb, L, nch, N, o_rw, 64, t0)

            if 'ml' in MIX_PARTS:
                e_, lp, cb_, al, be = mrow
                mf = i_mf[par]; mi = i_mi[par]
                fw.op("act", lambda h: h.activation(out=e_[:, :N], in_=mf[:, :N], func=AF.Exp, scale=-1.0, bias=der[:, 4:5]), R=[mf, der], W=[e_])
                fw.op("dve", lambda h: h.tensor_scalar(out=e_[:, :N], in0=e_[:, :N], scalar1=1.0, scalar2=None, op0=ALU.add), R=[e_], W=[e_])
                fw.op("act", lambda h: h.activation(out=lp[:, :N], in_=e_[:, :N], func=AF.Ln), R=[e_], W=[lp])
                for j in range(nch):
                    fw.op("dve", lambda h, j=j: h.tensor_tensor_scan(out=cb_[:, ch(j)], data0=ones64[:, 0:L], data1=lp[:, ch(j)], initial=0.0,
                                                                    op0=ALU.mult, op1=ALU.add), R=[ones64, lp], W=[cb_])
                fw.op("dve", lambda h: h.tensor_tensor(out=al[:, :N], in0=mi[:, :N], in1=cb_[:, :N], op=ALU.add), R=[mi, cb_], W=[al])
                fw.op("act", lambda h: h.activation(out=al[:, :N], in_=al[:, :N], func=AF.Exp, bias=P("i_b")), R=[al, prm], W=[al])
                fw.op("act", lambda h: h.activation(out=be[:, :N], in_=cb_[:, :N], func=AF.Exp, scale=-1.0), R=[cb_], W=[be])
                fw.op("dve", lambda h: h.tensor_tensor(out=kq[:, :N], in0=i_mk[par][:, :N], in1=al[:, :N], op=ALU.mult), R=[i_mk[par], al], W=[kq])
                fw.op("dve", lambda h: h.scalar_tensor_tensor(out=qq[:, :N], in0=i_mq[par][:, :N], scalar=0.125, in1=be[:, :N], op0=ALU.mult, op1=ALU.mult),
                      R=[i_mq[par], be], W=[qq])
                for j in range(nch):
                    ce = (j + 1) * L - 1
                    fw.op("dve", lambda h, j=j, ce=ce: h.tensor_copy(gam[:, j:j + 1], be[:, ce:ce + 1]), R=[be], W=[gam])
                for srct, dstt, dw, func in ((i_mv[par], vaug, 64, AF.Copy), (kq, ktok_m, 64, AF.Copy), (i_mo[par], otok, 64, AF.Sigmoid)):
                    pb = nb()
                    pb3 = b3(pb)
                    for j in range(nch):
                        fw.op("pe", lambda h, j=j: h.transpose(pb3[0:L, j, :], srct[:, ch(j)], ident[:, :]), R=[srct, ident], W=[pb])
                    fw.op("act", lambda h: h.activation(out=dstt[:L, :nch, 0:64], in_=pb3[0:L, :nch, :], func=func), R=[pb], W=[dstt])
                pb = nb()
                pb3 = b3(pb)
                for j in range(nch):
                    fw.op("pe", lambda h, j=j: h.matmul(pb3[0:L, j, 0:L], kq[:, ch(j)], qq[:, ch(j)], start=True, stop=True), R=[kq, qq], W=[pb])
                fw.op("dve", lambda h: h.tensor_tensor(out=STm[:L, :nch, :L], in0=pb3[0:L, :nch, 0:L], in1=mask_u8[:L, :nch, :L], op=ALU.mult), R=[pb, mask_u8], W=[STm])
                pn = [nb(), nb()]
                pn3 = [p_[:, 0:4 * 66].rearrange("p (a b) -> p a b", b=66) for p_ in pn]
                for j in range(nch):
                    Cc, Cn = Cm[hstate["C"]], Cm[1 - hstate["C"]]
                    pj, jj = pn[j // 4], j % 4
                    fw.op("pe", lambda h: h.matmul(pn3[j // 4][0:L, jj, :], STm[:L, j, :L], vaug[:L, j, :], start=True, stop=False), R=[STm, vaug], W=[pj])
                    fw.op("pe", lambda h: h.matmul(pn3[j // 4][0:L, jj, :], qq[:, ch(j)], Cc[:, :], start=False, stop=True), R=[qq, Cc], W=[pj])
                    fw.op("pe", lambda h: h.matmul(bk_seq[0:64, 192:258], ident[:, :], Cc[:, :], start=True, stop=False), R=[ident, Cc], W=[bk_seq])
                    fw.op("pe", lambda h: h.matmul(bk_seq[0:64, 192:258], ktok_m[:L, j, :], vaug[:L, j, :], start=False, stop=True), R=[ktok_m, vaug], W=[bk_seq])
                    fw.op("dve", lambda h: h.tensor_scalar(out=Cn[:, :], in0=bk_seq[0:64, 192:258], scalar1=gam[:, j:j + 1], scalar2=None, op0=ALU.mult),
                          R=[bk_seq, gam], W=[Cn])
                    hstate["C"] = 1 - hstate["C"]
                for half in range((nch + 3) // 4):
                    n4 = min(4, nch - half * 4)
                    fw.op("act", lambda h, half=half, n4=n4: h.activation(out=nd[:L, half * 4:half * 4 + n4, :], in_=pn3[half][0:L, 0:n4, :], func=AF.Copy),
                          R=[pn[half]], W=[nd])
                fw.op("dve", lambda h: h.tensor_scalar(out=gst[:L, :nch, 7], in0=nd[:L, :nch, 64], scalar1=-1.0, scalar2=None, op0=ALU.mult), R=[nd], W=[gst])
                fw.op("dve", lambda h: h.tensor_tensor(out=gst[:L, :nch, 7], in0=gst[:L, :nch, 7], in1=nd[:L, :nch, 64], op=ALU.max), R=[nd, gst], W=[gst])
                fw.op("dve", lambda h: h.tensor_scalar(out=gst[:L, :nch, 7], in0=gst[:L, :nch, 7], scalar1=1.0, scalar2=None, op0=ALU.max), R=[gst], W=[gst])
                fw.op("dve", lambda h: h.reciprocal(out=gst[:L, :nch, 7], in_=gst[:L, :nch, 7]), R=[gst], W=[gst])
                for j in range(nch):
                    fw.op("dve", lambda h, j=j: h.tensor_scalar(out=hh[:L, j, :], in0=nd[:L, j, 0:64], scalar1=gst[:L, j, 7:8], scalar2=None, op0=ALU.mult),
                          R=[nd, gst], W=[hh])
                group_norm(hh, L, nch, "ml_gng", "ml_gnb")
                fw.op("dve", lambda h: h.tensor_tensor(out=hh[:L, :nch, :], in0=hh[:L, :nch, :], in1=otok[:L, :nch, :], op=ALU.mult), R=[hh, otok], W=[hh])
                to_channel_major(hh, L, nch, N, o_ml, 192, t0)
        fw.finish()
    return nc


def head_cols(layer, hq):
    hc = np.arange(hq * 64, (hq + 1) * 64)
    rw = 256
    cols = [rw + hc, rw + 256 + hc, rw + 512 + hc]
    if layer > 0:
        cols.append(rw + 512 + np.arange(256))
    cols += [rw + 768 + np.arange(128), hc, 1152 + hc, 1408 + hc,
             1664 + hc, 1920 + hc, 2176 + hc, 2432 + hc, np.array([2688 + hq]), np.array([2692 + hq])]
    return np.concatenate(cols)


def wout_rows(hq):
    hc = np.arange(hq * 64, (hq + 1) * 64)
    return np.concatenate([m * 256 + hc for m in range(4)])


def pack_prm(inp, l, hq):
    a = np.zeros((128, NPRM), np.float32)
    hc = slice(hq * 64, (hq + 1) * 64)

    def put(name, arr, r0=0):
        o, w = PRM[name]
        arr = np.asarray(arr, np.float32)
        if arr.ndim == 1:
            arr = arr[:, None]
        a[r0:r0 + arr.shape[0], o:o + arr.shape[1]] = arr
    put("pw", inp["pool_w"][l, hq])
    put("pscale", inp["pool_scale"][l, hc])
    w = POOL_WINDOWS[hq]
    sel = np.zeros((64, 4), np.float32)
    sel[:, hq] = 1.0 / w
    put("psel", sel)
    put("pcnt", np.tile((w / np.minimum(np.arange(16) + 1.0, w))[None, :], (64, 1)))
    put("cw", inp["lru_conv_w"][l][:, hc].T)
    put("cb", inp["lru_conv_b"][l, hc])
    put("gaw", inp["lru_ga_w"][l, hq]); put("gab", inp["lru_ga_b"][l, hc])
    put("gxw", inp["lru_gx_w"][l, hq]); put("gxb", inp["lru_gx_b"][l, hc])
    put("lam", inp["lru_lambda"][l, hc])
    mu = inp["rw_mu"][l]
    put("mu_r", mu[0:256][hc]); put("mu_k", mu[256:512][hc]); put("mu_vh", mu[512:768][hc])
    put("mu_v", mu[512:768].reshape(2, 128).T); put("mu_l", mu[768:896])
    put("w0", inp["rw_w0"][l, hc]); put("a0", inp["rw_a0"][l, hc]); put("k_k", inp["rw_k_k"][l, hc]); put("k_a", inp["rw_k_a"][l, hc])
    put("r_k", inp["rw_r_k"][l, hq])
    put("lora", inp["rw_w_up"][l][:, hc], 0); put("lora", inp["rw_a_up"][l][:, hc], 32); put("lora", inp["rw_g_up"][l][:, hc], 64)
    if l > 0:
        put("v0", inp["rw_v0"][l - 1, hc])
        vd = inp["rw_v_down"][l - 1]
        put("vdown", np.concatenate([vd[0:128], vd[128:256]], axis=1))
        put("vup", inp["rw_v_up"][l - 1][:, hc])
    put("i_b", np.tile(inp["ml_if_b"][l, hq:hq + 1], 64)); put("f_b", np.tile(inp["ml_if_b"][l, 4 + hq:5 + hq], 64))
    put("rw_gng", np.tile(inp["rw_gn_g"][l, hc][None, :], (64, 1))); put("rw_gnb", np.tile(inp["rw_gn_b"][l, hc][None, :], (64, 1)))
    put("ml_gng", np.tile(inp["ml_gn_g"][l, hc][None, :], (64, 1))); put("ml_gnb", np.tile(inp["ml_gn_b"][l, hc][None, :], (64, 1)))
    return a


_PROGS = {}


def _prog(key, fn):
    if key not in _PROGS:
        _PROGS[key] = fn()
    return _PROGS[key]


def _fm(v):
    return np.asarray(v, np.float32).reshape(8, 128).T


def kernel(**inp):
    inp = {k: np.asarray(v) for k, v in inp.items()}
    x = inp["x"]
    B = x.shape[0]
    NX = SEQ // 4
    blocks = [(0, NMETA)] + [(NMETA + 512 * i, 512) for i in range(NX // 512)]
    cores = list(range(8))
    pcs = [pc_groups(l)[1] for l in range(DEPTH)]
    win = [np.ascontiguousarray(inp["w_in"][l][:, np.concatenate([head_cols(l, hq) for hq in range(4)])]) for l in range(DEPTH)]
    wperm = np.concatenate([wout_rows(hq) for hq in range(4)])
    zeros8 = np.zeros((128, 8), np.float32)

    in_maps = []
    lnp = np.concatenate([_fm(inp["emb_ln_g"]), _fm(inp["emb_ln_b"]), zeros8, zeros8], axis=1)
    for b in range(B):
        for c in range(4):
            xt = np.concatenate([inp["meta"], x[b, c * NX:(c + 1) * NX]], axis=0).T
            in_maps.append({"hT": np.ascontiguousarray(xt), "lnp": lnp, "w_in": win[0]})
    nc = _prog(("d", True, 4 * pcs[0]), lambda: build_dense(True, 4 * pcs[0], blocks, NTOK))
    res = run_bass_kernel_spmd(nc, in_maps, core_ids=cores).results
    hT = [r["hT_out"] for r in res]
    pTo = [r["pT_out"] for r in res]
    vf = None
    for l in range(DEPTH):
        PC = pcs[l]
        in_maps = []
        for b in range(B):
            for hq in range(4):
                parts = [pTo[b * 4][hq * PC:(hq + 1) * PC, 0:NMETA]] + [pTo[b * 4 + c][hq * PC:(hq + 1) * PC, NMETA:] for c in range(4)]
                m = {"pT": np.ascontiguousarray(np.concatenate(parts, axis=1)), "prm": pack_prm(inp, l, hq)}
                if l > 0:
                    m["vf_in"] = vf[b * 4 + hq]
                in_maps.append(m)
        nc = _prog(("m", l == 0), lambda: build_mixer(l == 0, SEQ // 512, TT))
        res = run_bass_kernel_spmd(nc, in_maps, core_ids=cores).results
        if l == 0:
            vf = [r["vf_out"] for r in res]
        yT = [r["yT"] for r in res]
        ncol = 4 * pcs[l + 1] if l + 1 < DEPTH else 0
        lnp = np.concatenate([_fm(inp["ln1_g"][l]), _fm(inp["ln1_b"][l]), _fm(inp["ln2_g"][l]), _fm(inp["ln2_b"][l])], axis=1)
        wo = np.ascontiguousarray(inp["w_out"][l][wperm])
        in_maps = []
        for b in range(B):
            for c in range(4):
                cols = np.concatenate([np.arange(NMETA), NMETA + c * NX + np.arange(NX)])
                y = np.concatenate([yT[b * 4 + hq][:, cols] for hq in range(4)], axis=0)
                m = {"hT": hT[b * 4 + c], "lnp": lnp, "yT": np.ascontiguousarray(y), "w_out": wo, "w1": inp["mlp_w1"][l], "w2": inp["mlp_w2"][l]}
                if ncol:
                    m["w_in"] = win[l + 1]
                in_maps.append(m)
        nc = _prog(("d", False, ncol), lambda: build_dense(False, ncol, blocks, NTOK))
        res = run_bass_kernel_spmd(nc, in_maps, core_ids=cores).results
        hT = [r["hT_out"] for r in res]
        if ncol:
            pTo = [r["pT_out"] for r in res]
    out = np.empty((B, SEQ, D), np.float32)
    for b in range(B):
        for c in range(4):
            out[b, c * NX:(c + 1) * NX] = hT[b * 4 + c][:, NMETA:].T
    return out
```
